# Optimizing a Trainium2 kernel written in Bass

```python
import jax, jax.numpy as jnp
from jax import lax
import numpy as np

D_MODEL = 1024
BATCH = 8
SEQ = 4096
DEPTH = 1

DA_HEADS = 4
DA_QK_DIM = 64
DA_V_DIM = 2 * DA_QK_DIM
DA_ROT_DIM = DA_QK_DIM // 4
ROPE_THETA = 500000.0
RET_HEADS = 4
RET_QK_DIM = 128
RET_V_DIM = 256
RET_THETA_BASE = 10000.0
D_FF = 4 * D_MODEL
N_BRANCH = 2
Q_BLOCK = 128
RET_CHUNK = 128
NORM_EPS = 1e-6
SUBLN_EPS = 1e-5
GN_EPS = 1e-5
MASK_VALUE = -1e30

DA_QK_W = DA_HEADS * 2 * DA_QK_DIM
DA_V_W = DA_HEADS * DA_V_DIM
RET_QK_W = RET_HEADS * RET_QK_DIM
RET_V_W = RET_HEADS * RET_V_DIM
GATE_W = N_BRANCH * D_MODEL
SPLIT_SIZES = (DA_QK_W, DA_QK_W, DA_V_W, RET_QK_W, RET_QK_W, RET_V_W, RET_V_W, GATE_W)
IN_W = sum(SPLIT_SIZES)
SPLIT_POINTS = tuple(int(v) for v in np.cumsum(SPLIT_SIZES)[:-1])

kernel_name = "hybrid_diffattn_retention_block"


def rms_norm(t, w, eps):
    tf = t.astype(jnp.float32)
    tf = tf * lax.rsqrt(jnp.mean(tf * tf, axis=-1, keepdims=True) + eps)
    return tf * w.astype(jnp.float32)


def rotate_half(t, cos, sin):
    t1, t2 = jnp.split(t, 2, axis=-1)
    return jnp.concatenate([t1 * cos - t2 * sin, t1 * sin + t2 * cos], axis=-1)


def diff_attention(q, k, v, lam):
    B, S, H, _, d = q.shape
    nb = S // Q_BLOCK
    qb = q.reshape(B, nb, Q_BLOCK, H, 2, d).transpose(1, 0, 2, 3, 4, 5)
    k_pos = jnp.arange(S)
    scale = DA_QK_DIM ** -0.5

    def block(args):
        q_blk, i = args
        q_pos = i * Q_BLOCK + jnp.arange(Q_BLOCK)
        s = jnp.einsum('bqhcd,bkhcd->bhcqk', q_blk, k) * scale
        causal = k_pos[None, :] <= q_pos[:, None]
        s = jnp.where(causal, s, MASK_VALUE)
        p = jax.nn.softmax(s, axis=-1)
        a = p[:, :, 0] - lam * p[:, :, 1]
        return jnp.einsum('bhqk,bkhe->bqhe', a, v)

    o = lax.map(block, (qb, jnp.arange(nb)))
    return o.transpose(1, 0, 2, 3, 4).reshape(B, S, H, v.shape[-1])


def retention_chunkwise(q, k, v, log_gamma):
    B, S, H, dk = q.shape
    dv = v.shape[-1]
    C = RET_CHUNK
    nc = S // C

    def to_chunks(t):
        return t.reshape(B, nc, C, H, t.shape[-1]).transpose(1, 0, 3, 2, 4)

    idx = jnp.arange(C, dtype=jnp.float32)
    lg = log_gamma[:, None, None]
    rel = idx[:, None] - idx[None, :]
    decay = jnp.where(rel >= 0, jnp.exp(lg * jnp.maximum(rel, 0.0)), 0.0)
    xi = jnp.exp(log_gamma[:, None] * (idx + 1.0))[None, :, :, None]
    zeta = jnp.exp(log_gamma[:, None] * (C - 1.0 - idx))[None, :, :, None]
    gamma_c = jnp.exp(log_gamma * C)[None, :, None, None]

    def step(state, chunk):
        qc, kc, vc = chunk
        scores = jnp.einsum('bhnd,bhmd->bhnm', qc, kc) * decay[None]
        inner = jnp.einsum('bhnm,bhme->bhne', scores, vc)
        cross = jnp.einsum('bhnd,bhde->bhne', qc, state) * xi
        new_state = gamma_c * state + jnp.einsum('bhmd,bhme->bhde', kc * zeta, vc)
        return new_state, inner + cross

    state0 = jnp.zeros((B, H, dk, dv), jnp.float32)
    _, out = lax.scan(step, state0, (to_chunks(q), to_chunks(k), to_chunks(v)))
    return out.transpose(1, 0, 3, 2, 4).reshape(B, S, H, dv)


def hybrid_layer(x, cos_da, sin_da, cos_rt, sin_rt, log_gamma, layer_idx,
                 norm1_w, w_in, q_norm_w, k_norm_w, lambda_q1, lambda_k1, lambda_q2, lambda_k2,
                 da_subln_w, w_da_branch, ret_gn_w, w_ret_branch, w_out,
                 norm2_w, w_mlp_in, w_mlp_out):
    B, S, _ = x.shape
    dt = x.dtype
    h = rms_norm(x, norm1_w, NORM_EPS).astype(dt)
    proj = h @ w_in
    dq, dk, dv, rq, rk, rv, rg, gl = jnp.split(proj, SPLIT_POINTS, axis=-1)

    lambda_init = 0.8 - 0.6 * float(np.exp(-0.3 * layer_idx))
    qa = rms_norm(dq.reshape(B, S, DA_HEADS, 2, DA_QK_DIM), q_norm_w, NORM_EPS)
    ka = rms_norm(dk.reshape(B, S, DA_HEADS, 2, DA_QK_DIM), k_norm_w, NORM_EPS)
    qa = jnp.concatenate([rotate_half(qa[..., :DA_ROT_DIM], cos_da, sin_da), qa[..., DA_ROT_DIM:]], axis=-1)
    ka = jnp.concatenate([rotate_half(ka[..., :DA_ROT_DIM], cos_da, sin_da), ka[..., DA_ROT_DIM:]], axis=-1)
    va = dv.reshape(B, S, DA_HEADS, DA_V_DIM).astype(jnp.float32)
    lam = (jnp.exp(jnp.sum(lambda_q1.astype(jnp.float32) * lambda_k1.astype(jnp.float32)))
           - jnp.exp(jnp.sum(lambda_q2.astype(jnp.float32) * lambda_k2.astype(jnp.float32)))
           + lambda_init)
    oa = diff_attention(qa, ka, va, lam)
    oa = rms_norm(oa, da_subln_w, SUBLN_EPS) * (1.0 - lambda_init)
    y_a = oa.reshape(B, S, DA_V_W).astype(dt) @ w_da_branch

    qr = rotate_half(rq.reshape(B, S, RET_HEADS, RET_QK_DIM).astype(jnp.float32), cos_rt, sin_rt)
    kr = rotate_half(rk.reshape(B, S, RET_HEADS, RET_QK_DIM).astype(jnp.float32), cos_rt, sin_rt)
    kr = kr * (RET_QK_DIM ** -0.5)
    vr = rv.reshape(B, S, RET_HEADS, RET_V_DIM).astype(jnp.float32)
    orr = retention_chunkwise(qr, kr, vr, log_gamma)
    mu = jnp.mean(orr, axis=-1, keepdims=True)
    var = jnp.mean(jnp.square(orr - mu), axis=-1, keepdims=True)
    orr = (orr - mu) * lax.rsqrt(var + GN_EPS) * ret_gn_w.astype(jnp.float32).reshape(RET_HEADS, RET_V_DIM)
    orr = jax.nn.silu(rg.astype(jnp.float32)) * orr.reshape(B, S, RET_V_W)
    y_r = orr.astype(dt) @ w_ret_branch

    g_a, g_r = jnp.split(gl, N_BRANCH, axis=-1)
    merged = jax.nn.sigmoid(g_a) * y_a + jax.nn.sigmoid(g_r) * y_r
    x = x + (merged.astype(dt) @ w_out)

    h2 = rms_norm(x, norm2_w, NORM_EPS).astype(dt)
    x = x + (jnp.square(jax.nn.relu(h2 @ w_mlp_in)) @ w_mlp_out).astype(dt)
    return x


def setup_inputs(seed: int = 0) -> dict:
    key = jax.random.key(seed)
    ks = jax.random.split(key, 20)
    f32 = jnp.float32

    def nrm(k, shape, scale):
        return jax.random.normal(k, shape, f32) * scale

    def gain(k, shape):
        return 1.0 + 0.02 * jax.random.normal(k, shape, f32)

    x = jax.random.normal(ks[0], (BATCH, SEQ, D_MODEL), f32)
    offsets = jax.random.randint(ks[1], (BATCH, 1), 0, 4096, dtype=jnp.int32)
    positions = (offsets + jnp.arange(SEQ, dtype=jnp.int32)[None, :]).astype(jnp.int32)
    return {
        "x": x,
        "positions": positions,
        "norm1_w": gain(ks[2], (DEPTH, D_MODEL)),
        "w_in": nrm(ks[3], (DEPTH, D_MODEL, IN_W), D_MODEL ** -0.5),
        "q_norm_w": gain(ks[4], (DEPTH, DA_QK_DIM)),
        "k_norm_w": gain(ks[5], (DEPTH, DA_QK_DIM)),
        "lambda_q1": nrm(ks[6], (DEPTH, DA_QK_DIM), 0.1),
        "lambda_k1": nrm(ks[7], (DEPTH, DA_QK_DIM), 0.1),
        "lambda_q2": nrm(ks[8], (DEPTH, DA_QK_DIM), 0.1),
        "lambda_k2": nrm(ks[9], (DEPTH, DA_QK_DIM), 0.1),
        "da_subln_w": gain(ks[10], (DEPTH, DA_V_DIM)),
        "w_da_branch": nrm(ks[11], (DEPTH, DA_V_W, D_MODEL), DA_V_W ** -0.5),
        "ret_gn_w": gain(ks[12], (DEPTH, RET_V_W)),
        "w_ret_branch": nrm(ks[13], (DEPTH, RET_V_W, D_MODEL), RET_V_W ** -0.5),
        "w_out": nrm(ks[14], (DEPTH, D_MODEL, D_MODEL), D_MODEL ** -0.5),
        "norm2_w": gain(ks[15], (DEPTH, D_MODEL)),
        "w_mlp_in": nrm(ks[16], (DEPTH, D_MODEL, D_FF), D_MODEL ** -0.5),
        "w_mlp_out": nrm(ks[17], (DEPTH, D_FF, D_MODEL), D_FF ** -0.5),
    }


def reference(x, positions, norm1_w, w_in, q_norm_w, k_norm_w, lambda_q1, lambda_k1,
              lambda_q2, lambda_k2, da_subln_w, w_da_branch, ret_gn_w, w_ret_branch,
              w_out, norm2_w, w_mlp_in, w_mlp_out):
    pos = positions.astype(jnp.float32)[..., None]
    inv_freq_da = ROPE_THETA ** (-jnp.arange(0, DA_ROT_DIM, 2, dtype=jnp.float32) / DA_ROT_DIM)
    ang_da = (pos * inv_freq_da)[:, :, None, None, :]
    cos_da, sin_da = jnp.cos(ang_da), jnp.sin(ang_da)
    inv_freq_rt = 1.0 / (RET_THETA_BASE ** jnp.linspace(0.0, 1.0, RET_QK_DIM // 2, dtype=jnp.float32))
    ang_rt = (pos * inv_freq_rt)[:, :, None, :]
    cos_rt, sin_rt = jnp.cos(ang_rt), jnp.sin(ang_rt)
    log_gamma = jnp.log1p(-jnp.exp2(-5.0 - jnp.arange(RET_HEADS, dtype=jnp.float32)))

    for l in range(DEPTH):
        x = hybrid_layer(x, cos_da, sin_da, cos_rt, sin_rt, log_gamma, l,
                         norm1_w[l], w_in[l], q_norm_w[l], k_norm_w[l],
                         lambda_q1[l], lambda_k1[l], lambda_q2[l], lambda_k2[l],
                         da_subln_w[l], w_da_branch[l], ret_gn_w[l], w_ret_branch[l],
                         w_out[l], norm2_w[l], w_mlp_in[l], w_mlp_out[l])
    return x
```

```python
import numpy as np
import concourse.bass as bass
import concourse.mybir as mybir
from concourse.bass_utils import run_bass_kernel_spmd

F32 = mybir.dt.float32
BF16 = mybir.dt.bfloat16
I32 = mybir.dt.int32
AF = mybir.ActivationFunctionType
ALU = mybir.AluOpType
AX = mybir.AxisListType

D = 1024
TT = 512
NSLOT = 3
TWO_PI = float(np.float32(2 * np.pi))
LAMBDA_INIT = 0.8 - 0.6 * float(np.exp(-0.3 * 0))


class Buf:
    __slots__ = ("ap", "w", "r", "ps")

    def __init__(self, ap, ps=False):
        self.ap = ap
        self.w = None
        self.r = {}
        self.ps = ps


class Sched:
    def __init__(self, nc):
        self.nc = nc
        self.eng = {"pe": nc.tensor, "act": nc.scalar, "dve": nc.vector,
                    "pool": nc.gpsimd, "sp": nc.sync}
        self.sems = {}
        self.cnt = {}
        self.waited = {e: {} for e in self.eng}
        self._stack = []
        for e in self.eng:
            self._new_sem("e_" + e)

    def _new_sem(self, key):
        cm = self.nc.semaphore(key)
        s = cm.__enter__()
        self._stack.append(cm)
        self.sems[key] = s
        self.cnt[key] = 0
        return key

    def dma_sem(self, name):
        return self._new_sem("d_" + name)

    def _deps(self, e, reads, writes, dma):
        own = "e_" + e
        need = {}

        def add(k, v):
            if need.get(k, 0) < v:
                need[k] = v
        for b in reads:
            if b.ps:
                for k_, v_ in b.r.items():
                    if k_ != own:
                        add(k_, v_)
            if b.w is not None:
                if b.w[0] == own and e == "pe":
                    continue
                add(*b.w)
        for b in writes:
            if b.w is not None and b.w[0] != dma and not (b.w[0] == own and e == "pe"):
                add(*b.w)
            for k_, v_ in b.r.items():
                if not (k_ == own and e == "pe"):
                    add(k_, v_)
        out = []
        for k, v in need.items():
            assert v <= self.cnt[k], f"wait on a not-yet-emitted signal {k}:{v}>{self.cnt[k]} (engine {e})"
            if self.waited[e].get(k, 0) < v:
                out.append((k, v))
                self.waited[e][k] = v
        return out

    def op(self, e, fn, reads=(), writes=(), inc=True, dma=None):
        eng = self.eng[e]
        waits = self._deps(e, reads, writes, dma)
        attach = None
        if waits and dma is None:
            attach = waits.pop()
        for k, v in waits:
            eng.wait_ge(self.sems[k], v)
        ins = fn(eng)
        if attach is not None:
            ins._wait_ge(self.sems[attach[0]], attach[1])
        if dma is not None:
            self.cnt[dma] += 16
            ins.then_inc(self.sems[dma], 16)
            tag = (dma, self.cnt[dma])
        else:
            own = "e_" + e
            if inc:
                self.cnt[own] += 1
                ins.then_inc(self.sems[own], 1)
                tag = (own, self.cnt[own])
            else:
                tag = (own, self.cnt[own] + 1)
        for b in reads:
            if b.r.get(tag[0], 0) < tag[1]:
                b.r[tag[0]] = tag[1]
        for b in writes:
            b.w = tag
            b.r = {}
        return ins


class Ring:
    def __init__(self, items):
        self.items = items
        self.i = 0

    def next(self):
        it = self.items[self.i % len(self.items)]
        self.i += 1
        return it


def host_consts():
    lg = np.log1p(-np.exp2(-5.0 - np.arange(4, dtype=np.float64)))
    C = 128
    idx = np.arange(C)
    mats = np.zeros((7, 128, 128), np.float32)
    mats[0] = np.eye(128)
    mats[1] = 1.0 / 1024
    mats[2][:64, :64] = 1.0 / 64
    mats[2][64:, 64:] = 1.0 / 64
    mats[3] = 1.0 / 128
    mats[4] = 1.0
    for g in (0, 64):
        for i in range(8):
            mats[5][g + i + 8, g + i] = -1.0
            mats[5][g + i, g + i + 8] = 1.0
    for p in range(64):
        mats[6][p + 64, p] = -1.0
        mats[6][p, p + 64] = 1.0
    c_mats = np.ascontiguousarray(mats.transpose(1, 0, 2).reshape(128, 7 * 128))
    k = idx[:, None, None]
    j = np.arange(4)[None, :, None]
    q = np.arange(512)[None, None, :]
    c_mask = ((128 * j + k) <= q).astype(np.float32).reshape(128, 2048)
    scale = 128 ** -0.5
    dec = np.zeros((128, 4, 128))
    xir = np.zeros((128, 4, 128))
    zeta = np.zeros((128, 4))
    for h in range(4):
        dec[:, h, :] = scale * np.exp(-lg[h] * (idx[:, None] + 1)) * (idx[None, :] >= idx[:, None])
        xir[:, h, :] = np.exp(lg[h] * (idx[None, :] + 1))
        zeta[:, h] = scale * np.exp(lg[h] * (C - 1 - idx))
    c_dec = np.concatenate([dec.reshape(128, 512), xir.reshape(128, 512)], axis=1).astype(np.float32)
    gamma_c = [float(v) for v in np.exp(lg * C)]
    invf_da = (np.float32(500000.0) ** (-np.arange(0, 16, 2, dtype=np.float32) / np.float32(16))).astype(np.float32)
    invf_rt = (np.float32(1.0) / (np.float32(10000.0) ** np.linspace(0.0, 1.0, 64, dtype=np.float32))).astype(np.float32)
    col_da = np.zeros(128, np.float32)
    col_rt = np.zeros(128, np.float32)
    for p in range(128):
        d = p % 64
        col_da[p] = invf_da[d % 8] if d < 16 else 0.0
        col_rt[p] = invf_rt[d]
    return dict(c_mats=c_mats, c_mask=c_mask, c_dec=c_dec, gamma_c=gamma_c,
                col_da=col_da, col_rt=col_rt, zeta=zeta.astype(np.float32))


def build(S, gamma_c, stop=None):
    NT = S // TT
    NKB = S // 128
    nc = bass.Bass("TRN2", target_bir_lowering=False)

    def din(name, shape, dt=F32):
        return nc.dram_tensor(name, shape, dt, kind="ExternalInput").ap()
    xT = din("xT", [D, S])
    pos = din("pos", [1, S], I32)
    w_in = din("w_in", [1024, 6656])
    w_a = din("w_a", [512, 1024])
    w_r = din("w_r", [1024, 1024])
    w_o = din("w_o", [1024, 1024])
    w1 = din("w1", [1024, 4096])
    w2 = din("w2", [4096, 1024])
    vecs = din("vecs", [128, 28])
    gnw = din("gnw", [1, 1024])
    lamv = din("lamv", [1, 256])
    c_mats = din("c_mats", [128, 7 * 128])
    c_mask = din("c_mask", [128, 2048])
    c_dec = din("c_dec", [128, 1024])
    outT = nc.dram_tensor("outT", [D, S], F32, kind="ExternalOutput").ap()

    X = Sched(nc)
    keep = []

    def sb(name, shape, dt=F32):
        cm = nc.sbuf_tensor(name, shape, dt)
        t = cm.__enter__()
        keep.append(cm)
        return t

    def pb(name, shape, dt=F32):
        cm = nc.psum_tensor(name, shape, dt)
        t = cm.__enter__()
        keep.append(cm)
        return t

    KT = sb("KT", [128, 4, S], BF16)
    KTb = [Buf(KT) for _ in range(NT)]
    Vst = sb("Vst", [128, NKB, 512], BF16)
    Vb = [Buf(Vst) for _ in range(NT)]
    xTs = sb("xTs", [128, 8, TT], F32)
    XT = Buf(xTs)
    hT = sb("hT", [128, 8, TT], BF16)
    HT = Buf(hT)
    RBt = sb("RB", [128, 32, TT], BF16)
    RB = [Buf(RBt) for _ in range(32)]
    oaT = sb("oaT", [128, 4, TT], BF16)
    OAT = Buf(oaT)
    orrT = sb("orrT", [128, 8, TT], BF16)
    ORRT = Buf(orrT)
    orrtok = sb("orrtok", [128, 1024], BF16)
    ORRTOK = Buf(orrtok)
    slots = [sb(f"slot{i}", [128, 4096], BF16) for i in range(NSLOT)]
    SLOT = [Buf(s) for s in slots]
    d_slot = [X.dma_sem(f"slot{i}") for i in range(NSLOT)]
    Tg = Ring([Buf(sb(f"tg{i}", [128, TT], F32)) for i in range(5)])
    Pr = Ring([Buf(sb(f"pr{i}", [128, TT], BF16)) for i in range(4)])
    O0 = Buf(sb("o0", [128, TT], F32))
    tab = sb("tab", [128, 4, TT], F32)
    TAB = [Buf(tab) for _ in range(4)]
    cm = sb("cm", [128, 7, 128], BF16)
    CM = Buf(cm)
    cmask = sb("cmask", [128, 4, TT], BF16)
    CMASK = Buf(cmask)
    dec = sb("dec", [128, 2, 4, 128], F32)
    DEC = Buf(dec)
    gnw_sb = sb("gnw_sb", [128, 1024], F32)
    GNW = Buf(gnw_sb)
    vec = sb("vec", [128, 28], F32)
    VEC = Buf(vec)
    lam_sb = sb("lam_sb", [128, 256], F32)
    LAM = Buf(lam_sb)
    sm = sb("sm", [128, 8], F32)
    SM = Buf(sm)
    stf = sb("stf", [128, 4, 256], F32)
    STF = [Buf(stf) for _ in range(4)]
    stb = sb("stb", [128, 4, 256], BF16)
    STB = [Buf(stb) for _ in range(4)]
    st6 = sb("st6", [128, 4, 6], F32)
    ST6 = [Buf(st6) for _ in range(4)]
    mv = sb("mv", [128, 4, 2], F32)
    MV = [Buf(mv) for _ in range(4)]
    rsd = sb("rsd", [128, 4, 2], F32)
    RSD = [Buf(rsd) for _ in range(4)]
    ps = [pb(f"ps{i}", [128, 512], F32) for i in range(8)]
    PS = [Buf(p, ps=True) for p in ps]
    psb = [p[:].bitcast(BF16) for p in ps]

    d_setup = X.dma_sem("setup")
    d_setup_p = X.dma_sem("setup_p")
    d_x = X.dma_sem("x")
    d_pos = X.dma_sem("pos")
    d_o = [X.dma_sem(f"o{i}") for i in range(8)]

    def v8(i):
        return slots[i][:].rearrange("p (c n) -> p c n", c=8)

    setup_bufs = []

    def sload(eng, out_ap, in_ap, b):
        X.op(eng, lambda e: e.dma_start(out=out_ap, in_=in_ap), writes=[b], dma=(d_setup if eng == "sp" else d_setup_p))
        setup_bufs.append((b, d_setup if eng == "sp" else d_setup_p))
    sload("pool", cm[:], c_mats.rearrange("p (a b) -> p a b", a=7), CM)
    sload("pool", cmask[:], c_mask.rearrange("p (a b) -> p a b", a=4), CMASK)
    sload("sp", dec[:], c_dec.rearrange("p (t h n) -> p t h n", t=2, h=4), DEC)
    sload("sp", vec[:], vecs[:, :], VEC)
    sload("sp", gnw_sb[:], gnw[0:1, :].partition_broadcast(128), GNW)
    sload("sp", lam_sb[:], lamv[0:1, :].partition_broadcast(128), LAM)
    for b, k_ in setup_bufs:
        b.w = (k_, X.cnt[k_])

    for h in range(4):
        X.op("dve", lambda e: e.memset(stf[:, h, :], 0.0), writes=[STF[h]])
        X.op("dve", lambda e: e.memset(stb[:, h, :], 0.0), writes=[STB[h]])
    tl = Tg.next()
    X.op("dve", lambda e: e.tensor_tensor(out=tl.ap[:, 0:64], in0=lam_sb[:, 0:64], in1=lam_sb[:, 64:128], op=ALU.mult), reads=[LAM], writes=[tl])
    X.op("dve", lambda e: e.tensor_tensor(out=tl.ap[:, 64:128], in0=lam_sb[:, 128:192], in1=lam_sb[:, 192:256], op=ALU.mult), reads=[LAM], writes=[tl])
    X.op("dve", lambda e: e.reduce_sum(out=sm[:, 0:1], in_=tl.ap[:, 0:64], axis=AX.X), reads=[tl], writes=[SM])
    X.op("dve", lambda e: e.reduce_sum(out=sm[:, 1:2], in_=tl.ap[:, 64:128], axis=AX.X), reads=[tl, SM], writes=[SM])
    X.op("act", lambda e: e.activation(out=sm[:, 2:4], in_=sm[:, 0:2], func=AF.Exp), reads=[SM], writes=[SM])
    X.op("dve", lambda e: e.tensor_tensor(out=sm[:, 4:5], in0=sm[:, 3:4], in1=sm[:, 2:3], op=ALU.subtract), reads=[SM], writes=[SM])
    X.op("dve", lambda e: e.tensor_scalar(out=sm[:, 4:5], in0=sm[:, 4:5], scalar1=-LAMBDA_INIT, scalar2=None, op0=ALU.add), reads=[SM], writes=[SM])
    X.op("dve", lambda e: e.tensor_scalar(out=sm[:, 5:6], in0=vec[:, 18:19], scalar1=1.0 - LAMBDA_INIT, scalar2=None, op0=ALU.mult), reads=[SM, VEC], writes=[SM])

    def finish():
        for dc in range(8):
            X.op("sp", lambda e: e.dma_start(out=outT[dc * 128:(dc + 1) * 128, 0:TT], in_=xTs[:, dc, :]), reads=[XT], dma=d_o[dc])
        for k in d_o:
            nc.sync.wait_ge(X.sems[k], X.cnt[k])
        return nc

    if stop == -1:
        X.op("sp", lambda e: e.dma_start(out=xTs[:], in_=xT[:, 0:TT].rearrange("(c p) s -> p c s", p=128)), writes=[XT], dma=d_x)
        return finish()
    EPS6 = vec[:, 19:20]
    EPS5 = vec[:, 20:21]
    HALFPI = vec[:, 21:22]

    slot_ctr = [0]

    def wload(parts):
        i = slot_ctr[0] % NSLOT
        slot_ctr[0] += 1
        for dst, src in parts:
            X.op("pool", lambda e: e.dma_start(out=dst(i), in_=src), writes=[SLOT[i]], dma=d_slot[i])
        return i

    def wload8(w, c0):
        return wload([(lambda i: v8(i), w[:, c0:c0 + 512].rearrange("(c p) n -> p c n", p=128))])

    def mm(out, lhsT, rhs, start, stop, reads, writes, inc):
        X.op("pe", lambda e: e.matmul(out, lhsT, rhs, start=start, stop=stop), reads=reads, writes=writes, inc=inc)

    def rstd_from(psbuf, psap, epsap):
        sd = Tg.next()
        X.op("act", lambda e: e.activation(out=sd.ap[:], in_=psap, func=AF.Sqrt, bias=epsap, scale=1.0), reads=[psbuf, VEC], writes=[sd])
        rs = Tg.next()
        X.op("dve", lambda e: e.reciprocal(out=rs.ap[:], in_=sd.ap[:]), reads=[sd], writes=[rs])
        return rs

    def rmsnorm(wbase):
        for c in range(8):
            sq = Pr.next()
            X.op("act", lambda e: e.activation(out=sq.ap[:], in_=xTs[:, c, :], func=AF.Square), reads=[XT], writes=[sq])
            mm(ps[7][:], cm[:, 1, :], sq.ap[:], c == 0, c == 7, [CM, sq], [PS[7]], True)
        rs = rstd_from(PS[7], ps[7][:], EPS6)
        for c in range(8):
            X.op("dve", lambda e: e.scalar_tensor_tensor(out=hT[:, c, :], in0=xTs[:, c, :], scalar=vec[:, wbase + c:wbase + c + 1], in1=rs.ap[:], op0=ALU.mult, op1=ALU.mult), reads=[XT, VEC, rs], writes=[HT])

    def fm_chunk(si, j, bank):
        for c in range(8):
            mm(ps[bank][:], v8(si)[:, c, j * 128:(j + 1) * 128], hT[:, c, :], c == 0, c == 7, [SLOT[si], HT], [PS[bank]], c == 7)

    def tm_chunk(si, i, bank):
        for c in range(8):
            mm(ps[bank][:], hT[:, c, i * 128:(i + 1) * 128], v8(si)[:, c, :], c == 0, c == 7, [SLOT[si], HT], [PS[bank]], c == 7)

    ringA = Ring([0, 1, 2])
    ringM = Ring([3, 4])
    ringP = Ring([5, 6])

    for t in range(NT):
        t0 = t * TT
        tsl = slice(t0, t0 + TT)
        X.op("sp", lambda e: e.dma_start(out=xTs[:], in_=xT[:, tsl].rearrange("(c p) s -> p c s", p=128)), writes=[XT], dma=d_x)
        pi_ = Tg.next()
        X.op("sp", lambda e: e.dma_start(out=pi_.ap[:].bitcast(I32), in_=pos[0:1, tsl].partition_broadcast(128)), writes=[pi_], dma=d_pos)
        pf = O0
        X.op("dve", lambda e: e.tensor_copy(out=pf.ap[:], in_=pi_.ap[:].bitcast(I32)), reads=[pi_], writes=[pf])
        for tb, col in ((0, 22), (2, 23)):
            ang = Tg.next()
            X.op("dve", lambda e: e.tensor_scalar(out=ang.ap[:], in0=pf.ap[:], scalar1=vec[:, col:col + 1], scalar2=None, op0=ALU.mult), reads=[pf, VEC], writes=[ang])
            for which in (1, 0):
                u = Tg.next()
                if which == 1:
                    X.op("dve", lambda e: e.tensor_scalar(out=u.ap[:], in0=ang.ap[:], scalar1=float(1.0 / (2 * np.pi)), scalar2=None, op0=ALU.mult), reads=[ang], writes=[u])
                else:
                    X.op("dve", lambda e: e.tensor_scalar(out=u.ap[:], in0=ang.ap[:], scalar1=float(1.0 / (2 * np.pi)), scalar2=0.25, op0=ALU.mult, op1=ALU.add), reads=[ang], writes=[u])
                ki = Tg.next()
                X.op("dve", lambda e: e.tensor_copy(out=ki.ap[:].bitcast(I32), in_=u.ap[:]), reads=[u], writes=[ki])
                X.op("dve", lambda e: e.tensor_copy(out=u.ap[:], in_=ki.ap[:].bitcast(I32)), reads=[ki], writes=[u])
                X.op("dve", lambda e: e.scalar_tensor_tensor(out=ki.ap[:], in0=u.ap[:], scalar=-TWO_PI, in1=ang.ap[:], op0=ALU.mult, op1=ALU.add), reads=[u, ang], writes=[ki])
                if which == 1:
                    X.op("dve", lambda e: e.tensor_scalar(out=ki.ap[:], in0=ki.ap[:], scalar1=-3.141592, scalar2=3.141592, op0=ALU.max, op1=ALU.min), reads=[ki], writes=[ki])
                    X.op("act", lambda e: e.activation(out=tab[:, tb + 1, :], in_=ki.ap[:], func=AF.Sin), reads=[ki], writes=[TAB[tb + 1]])
                else:
                    X.op("dve", lambda e: e.tensor_scalar(out=ki.ap[:], in0=ki.ap[:], scalar1=-4.712388, scalar2=1.570796, op0=ALU.max, op1=ALU.min), reads=[ki], writes=[ki])
                    X.op("act", lambda e: e.activation(out=tab[:, tb, :], in_=ki.ap[:], func=AF.Sin, bias=HALFPI), reads=[ki, VEC], writes=[TAB[tb]])

        if stop == 0:
            return finish()
        rmsnorm(0)

        if stop == 1:
            return finish()
        def da_stage_a(si, j):
            bq = ringA.next()
            fm_chunk(si, j, bq)
            sq = Pr.next()
            X.op("act", lambda e: e.activation(out=sq.ap[:], in_=ps[bq][:], func=AF.Square), reads=[PS[bq]], writes=[sq])
            return dict(bq=bq, sq=sq, j=j)

        def da_stage_b(st, wcol):
            bm = ringM.next()
            mm(ps[bm][:], cm[:, 2, :], st["sq"].ap[:], True, True, [CM, st["sq"]], [PS[bm]], True)
            rs = rstd_from(PS[bm], ps[bm][:], EPS6)
            qn = Pr.next()
            bq = st["bq"]
            X.op("dve", lambda e: e.scalar_tensor_tensor(out=qn.ap[:], in0=ps[bq][:], scalar=vec[:, wcol:wcol + 1], in1=rs.ap[:], op0=ALU.mult, op1=ALU.mult), reads=[PS[bq], VEC, rs], writes=[qn])
            st["qn"] = qn

        def da_stage_c(st, dst_ap, dst_buf):
            bp = ringP.next()
            qn = st["qn"]
            mm(ps[bp][:], cm[:, 5, :], qn.ap[:], True, True, [CM, qn], [PS[bp]], True)
            t1 = Tg.next()
            X.op("dve", lambda e: e.tensor_tensor(out=t1.ap[:], in0=qn.ap[:], in1=tab[:, 0, :], op=ALU.mult), reads=[qn, TAB[0]], writes=[t1])
            t2 = Tg.next()
            X.op("dve", lambda e: e.tensor_tensor(out=t2.ap[:], in0=ps[bp][:], in1=tab[:, 1, :], op=ALU.mult), reads=[PS[bp], TAB[1]], writes=[t2])
            X.op("dve", lambda e: e.tensor_tensor(out=dst_ap, in0=t1.ap[:], in1=t2.ap[:], op=ALU.add), reads=[t1, t2], writes=[dst_buf])

        for kind in ("q", "k"):
            si = wload8(w_in, 0 if kind == "q" else 512)
            wcol = 16 if kind == "q" else 17
            sts = {}
            sts[0] = da_stage_a(si, 0)
            for j in range(4):
                if j + 1 < 4:
                    sts[j + 1] = da_stage_a(si, j + 1)
                if j >= 1:
                    jj = j - 1
                    if kind == "q":
                        da_stage_c(sts[jj], RBt[:, jj, :], RB[jj])
                    else:
                        da_stage_c(sts[jj], KT[:, jj, tsl], KTb[t])
                da_stage_b(sts[j], wcol)
            if kind == "q":
                da_stage_c(sts[3], RBt[:, 3, :], RB[3])
            else:
                da_stage_c(sts[3], KT[:, 3, tsl], KTb[t])

        if stop == 2:
            return finish()
        si = wload8(w_in, 1024)
        for i in range(4):
            b = ringA.next()
            tm_chunk(si, i, b)
            X.op("act", lambda e: e.activation(out=Vst[:, 4 * t + i, :], in_=ps[b][:], func=AF.Copy), reads=[PS[b]], writes=[Vb[t]])

        if stop == 21:
            return finish()
        def rt_stage_a(si, j):
            bq = ringA.next()
            fm_chunk(si, j, bq)
            qb = Pr.next()
            X.op("act", lambda e: e.activation(out=qb.ap[:], in_=ps[bq][:], func=AF.Copy), reads=[PS[bq]], writes=[qb])
            return dict(bq=bq, qb=qb, j=j)

        def rt_stage_c(st, kind):
            j = st["j"]
            bq = st["bq"]
            qb = st["qb"]
            bp = ringP.next()
            mm(ps[bp][:], cm[:, 6, :], qb.ap[:], True, True, [CM, qb], [PS[bp]], True)
            if stop == 2241:
                return "stop"
            t1 = Tg.next()
            X.op("dve", lambda e: e.tensor_tensor(out=t1.ap[:], in0=ps[bq][:], in1=tab[:, 2, :], op=ALU.mult), reads=[PS[bq], TAB[2]], writes=[t1])
            if stop == 2242:
                return "stop"
            t2 = Tg.next()
            X.op("dve", lambda e: e.tensor_tensor(out=t2.ap[:], in0=ps[bp][:], in1=tab[:, 3, :], op=ALU.mult), reads=[PS[bp], TAB[3]], writes=[t2])
            if kind == "q":
                qr = Tg.next()
                X.op("dve", lambda e: e.tensor_tensor(out=qr.ap[:], in0=t1.ap[:], in1=t2.ap[:], op=ALU.add), reads=[t1, t2], writes=[qr])
                if stop == 224:
                    return "stop"
                for n in range(4):
                    cs = slice(n * 128, (n + 1) * 128)
                    X.op("dve", lambda e: e.tensor_tensor(out=RBt[:, 4 + j, cs], in0=qr.ap[:, cs], in1=dec[:, 1, j, :], op=ALU.mult), reads=[qr, DEC], writes=[RB[4 + j]])
            else:
                X.op("dve", lambda e: e.tensor_tensor(out=RBt[:, 8 + j, :], in0=t1.ap[:], in1=t2.ap[:], op=ALU.add), reads=[t1, t2], writes=[RB[8 + j]])
                bt = ringM.next()
                for n in range(4):
                    cs = slice(n * 128, (n + 1) * 128)
                    X.op("pe", lambda e: e.transpose(psb[bt][:, cs], RBt[:, 8 + j, cs], cm[:, 0, :]), reads=[RB[8 + j], CM], writes=[PS[bt]], inc=(n == 3))
                X.op("dve", lambda e: e.tensor_scalar(out=RBt[:, 12:16, j * 128:(j + 1) * 128], in0=psb[bt][:, 0:512].rearrange("p (n d) -> p n d", n=4), scalar1=vec[:, 24 + j:25 + j], scalar2=None, op0=ALU.mult), reads=[PS[bt], VEC], writes=[RB[12], RB[13], RB[14], RB[15]])

        for kind in ("q", "k"):
            si = wload8(w_in, 1536 if kind == "q" else 2048)
            if stop == 221:
                return finish()
            sts = {}
            sts[0] = rt_stage_a(si, 0)
            if stop == 222:
                return finish()
            for j in range(4):
                if j + 1 < 4:
                    sts[j + 1] = rt_stage_a(si, j + 1)
                if stop == 223:
                    return finish()
                if rt_stage_c(sts[j], kind) == "stop":
                    return finish()
            if stop == 22 and kind == "q":
                return finish()

        if stop == 23:
            return finish()
        for half in range(2):
            si = wload8(w_in, 2560 + half * 512)
            for i in range(4):
                b = ringA.next()
                tm_chunk(si, i, b)
                X.op("act", lambda e: e.activation(out=RBt[:, 16 + 2 * i + half, :], in_=ps[b][:], func=AF.Copy), reads=[PS[b]], writes=[RB[16 + 2 * i + half]])
        for half in range(2):
            si = wload8(w_in, 3584 + half * 512)
            for i in range(4):
                b = ringA.next()
                tm_chunk(si, i, b)
                X.op("act", lambda e: e.activation(out=RBt[:, 24 + 2 * i + half, :], in_=ps[b][:], func=AF.Silu), reads=[PS[b]], writes=[RB[24 + 2 * i + half]])

        if stop == 3:
            return finish()
        nkb = 4 * t + 4
        for hh in range(4):
            for c in range(2):
                u = hh * 2 + c
                bo, bz = (2, 3) if u % 2 == 0 else (4, 5)
                rows = slice(c * 64, (c + 1) * 64)

                def qk(kb):
                    bs = kb % 2
                    mm(ps[bs][:], KT[rows, hh, kb * 128:(kb + 1) * 128], RBt[rows, hh, :], True, True, [KTb[kb // 4], RB[hh]], [PS[bs]], True)

                qk(0)
                for kb in range(nkb):
                    if kb + 1 < nkb:
                        qk(kb + 1)
                    bs = kb % 2
                    p = Pr.next()
                    X.op("act", lambda e: e.activation(out=p.ap[:], in_=ps[bs][:], func=AF.Exp, scale=0.125), reads=[PS[bs]], writes=[p])
                    if kb >= 4 * t:
                        X.op("dve", lambda e: e.tensor_tensor(out=p.ap[:], in0=p.ap[:], in1=cmask[:, kb - 4 * t, :], op=ALU.mult), reads=[p, CMASK], writes=[p])
                    mm(ps[bo][:], Vst[:, kb, hh * 128:(hh + 1) * 128], p.ap[:], kb == 0, kb == nkb - 1, [Vb[kb // 4], p], [PS[bo]], False)
                    mm(ps[bz][:], cm[:, 4, :], p.ap[:], kb == 0, kb == nkb - 1, [CM, p], [PS[bz]], True)
                rz = Tg.next()
                X.op("dve", lambda e: e.reciprocal(out=rz.ap[:], in_=ps[bz][:]), reads=[PS[bz]], writes=[rz])
                if c == 0:
                    X.op("dve", lambda e: e.tensor_tensor(out=O0.ap[:], in0=ps[bo][:], in1=rz.ap[:], op=ALU.mult), reads=[PS[bo], rz], writes=[O0])
                else:
                    u1 = Tg.next()
                    X.op("dve", lambda e: e.tensor_tensor(out=u1.ap[:], in0=ps[bo][:], in1=rz.ap[:], op=ALU.mult), reads=[PS[bo], rz], writes=[u1])
                    o = Tg.next()
                    X.op("dve", lambda e: e.scalar_tensor_tensor(out=o.ap[:], in0=u1.ap[:], scalar=sm[:, 4:5], in1=O0.ap[:], op0=ALU.mult, op1=ALU.add), reads=[u1, SM, O0], writes=[o])
                    osq = Pr.next()
                    X.op("act", lambda e: e.activation(out=osq.ap[:], in_=o.ap[:], func=AF.Square), reads=[o], writes=[osq])
                    mm(ps[6][:], cm[:, 3, :], osq.ap[:], True, True, [CM, osq], [PS[6]], True)
                    rs = rstd_from(PS[6], ps[6][:], EPS5)
                    X.op("dve", lambda e: e.scalar_tensor_tensor(out=oaT[:, hh, :], in0=o.ap[:], scalar=sm[:, 5:6], in1=rs.ap[:], op0=ALU.mult, op1=ALU.mult), reads=[o, SM, rs], writes=[OAT])

        if stop == 4:
            return finish()
        for n in range(4):
            cs = slice(n * 128, (n + 1) * 128)
            for h in range(4):
                bsc = h % 2
                bab = 2 + h % 2
                bu = 4 + h % 2
                vsl = RBt[:, 16 + 2 * n + h // 2, (h % 2) * 256:(h % 2 + 1) * 256]
                VB = RB[16 + 2 * n + h // 2]
                mm(ps[bsc][:, 0:128], RBt[:, 8 + h, cs], RBt[:, 4 + h, cs], True, True, [RB[8 + h], RB[4 + h]], [PS[bsc]], True)
                scm = Pr.next()
                X.op("dve", lambda e: e.tensor_tensor(out=scm.ap[:, 0:128], in0=ps[bsc][:, 0:128], in1=dec[:, 0, h, :], op=ALU.mult), reads=[PS[bsc], DEC], writes=[scm])
                mm(ps[bab][:, 0:256], scm.ap[:, 0:128], vsl, True, False, [scm, VB], [PS[bab]], False)
                mm(ps[bab][:, 0:256], RBt[:, 4 + h, cs], stb[:, h, :], False, True, [RB[4 + h], STB[h]], [PS[bab]], True)
                mm(ps[bu][:, 0:256], RBt[:, 12 + n, h * 128:(h + 1) * 128], vsl, True, True, [RB[12 + n], VB], [PS[bu]], True)
                X.op("dve", lambda e: e.scalar_tensor_tensor(out=stf[:, h, :], in0=stf[:, h, :], scalar=gamma_c[h], in1=ps[bu][:, 0:256], op0=ALU.mult, op1=ALU.add), reads=[STF[h], PS[bu]], writes=[STF[h]])
                X.op("act", lambda e: e.activation(out=stb[:, h, :], in_=stf[:, h, :], func=AF.Copy), reads=[STF[h]], writes=[STB[h]])
                X.op("dve", lambda e: e.bn_stats(out=st6[:, h, :], in_=ps[bab][:, 0:256]), reads=[PS[bab]], writes=[ST6[h]])
                X.op("dve", lambda e: e.bn_aggr(out=mv[:, h, :], in_=st6[:, h, :]), reads=[ST6[h]], writes=[MV[h]])
                X.op("act", lambda e: e.activation(out=rsd[:, h, 0:1], in_=mv[:, h, 1:2], func=AF.Sqrt, bias=EPS5, scale=1.0), reads=[MV[h], VEC], writes=[RSD[h]])
                X.op("dve", lambda e: e.reciprocal(out=rsd[:, h, 1:2], in_=rsd[:, h, 0:1]), reads=[RSD[h]], writes=[RSD[h]])
                on = Tg.next()
                X.op("dve", lambda e: e.tensor_scalar(out=on.ap[:, 0:256], in0=ps[bab][:, 0:256], scalar1=mv[:, h, 0:1], scalar2=rsd[:, h, 1:2], op0=ALU.subtract, op1=ALU.mult), reads=[PS[bab], MV[h], RSD[h]], writes=[on])
                g1 = Tg.next()
                X.op("dve", lambda e: e.tensor_tensor(out=g1.ap[:, 0:256], in0=on.ap[:, 0:256], in1=gnw_sb[:, h * 256:(h + 1) * 256], op=ALU.mult), reads=[on, GNW], writes=[g1])
                X.op("dve", lambda e: e.tensor_tensor(out=orrtok[:, h * 256:(h + 1) * 256], in0=g1.ap[:, 0:256], in1=RBt[:, 24 + 2 * n + h // 2, (h % 2) * 256:(h % 2 + 1) * 256], op=ALU.mult), reads=[g1, RB[24 + 2 * n + h // 2]], writes=[ORRTOK])
            for c8 in range(8):
                X.op("pe", lambda e: e.transpose(psb[6][:, c8 * 128:(c8 + 1) * 128], orrtok[:, c8 * 128:(c8 + 1) * 128], cm[:, 0, :]), reads=[ORRTOK, CM], writes=[PS[6]], inc=(c8 == 7))
            X.op("act", lambda e: e.activation(out=orrT[:, :, cs], in_=psb[6][:, :].rearrange("p (c n) -> p c n", c=8), func=AF.Copy), reads=[PS[6]], writes=[ORRT])

        if stop == 5:
            return finish()
        for qd in range(4):
            c0 = qd * 256
            sA = wload([
                (lambda i: slots[i][:, 0:1024].rearrange("p (c n) -> p c n", c=4), w_a[:, c0:c0 + 256].rearrange("(c p) n -> p c n", p=128)),
                (lambda i: slots[i][:, 1024:3072].rearrange("p (c n) -> p c n", c=8), w_r[:, c0:c0 + 256].rearrange("(c p) n -> p c n", p=128)),
            ])
            sB = wload([
                (lambda i: slots[i][:, 0:2048].rearrange("p (c n) -> p c n", c=8), w_in[:, 4608 + c0:4608 + c0 + 256].rearrange("(c p) n -> p c n", p=128)),
                (lambda i: slots[i][:, 2048:4096].rearrange("p (c n) -> p c n", c=8), w_in[:, 5632 + c0:5632 + c0 + 256].rearrange("(c p) n -> p c n", p=128)),
            ])
            wa_v = slots[sA][:, 0:1024].rearrange("p (c n) -> p c n", c=4)
            wr_v = slots[sA][:, 1024:3072].rearrange("p (c n) -> p c n", c=8)
            ga_v = slots[sB][:, 0:2048].rearrange("p (c n) -> p c n", c=8)
            gr_v = slots[sB][:, 2048:4096].rearrange("p (c n) -> p c n", c=8)
            for dd in range(2):
                dc = 2 * qd + dd
                b0 = 4 * (dc % 2)
                ds = slice(dd * 128, (dd + 1) * 128)
                for c in range(8):
                    mm(ps[b0 + 2][:], ga_v[:, c, ds], hT[:, c, :], c == 0, c == 7, [SLOT[sB], HT], [PS[b0 + 2]], c == 7)
                for c in range(8):
                    mm(ps[b0 + 3][:], gr_v[:, c, ds], hT[:, c, :], c == 0, c == 7, [SLOT[sB], HT], [PS[b0 + 3]], c == 7)
                for c in range(4):
                    mm(ps[b0][:], wa_v[:, c, ds], oaT[:, c, :], c == 0, c == 3, [SLOT[sA], OAT], [PS[b0]], c == 3)
                for c in range(8):
                    mm(ps[b0 + 1][:], wr_v[:, c, ds], orrT[:, c, :], c == 0, c == 7, [SLOT[sA], ORRT], [PS[b0 + 1]], c == 7)
                sa = Tg.next()
                X.op("act", lambda e: e.activation(out=sa.ap[:], in_=ps[b0 + 2][:], func=AF.Sigmoid), reads=[PS[b0 + 2]], writes=[sa])
                sr = Tg.next()
                X.op("act", lambda e: e.activation(out=sr.ap[:], in_=ps[b0 + 3][:], func=AF.Sigmoid), reads=[PS[b0 + 3]], writes=[sr])
                X.op("dve", lambda e: e.tensor_tensor(out=sa.ap[:], in0=ps[b0][:], in1=sa.ap[:], op=ALU.mult), reads=[PS[b0], sa], writes=[sa])
                X.op("dve", lambda e: e.tensor_tensor(out=sr.ap[:], in0=ps[b0 + 1][:], in1=sr.ap[:], op=ALU.mult), reads=[PS[b0 + 1], sr], writes=[sr])
                X.op("dve", lambda e: e.tensor_tensor(out=RBt[:, dc, :], in0=sa.ap[:], in1=sr.ap[:], op=ALU.add), reads=[sa, sr], writes=[RB[dc]])

        if stop == 6:
            return finish()
        ringO = Ring([0, 1, 2, 3])
        for half in range(2):
            si = wload8(w_o, half * 512)
            for j in range(4):
                dc = half * 4 + j
                b = ringO.next()
                for c in range(8):
                    mm(ps[b][:], v8(si)[:, c, j * 128:(j + 1) * 128], RBt[:, c, :], c == 0, c == 7, [SLOT[si], RB[c]], [PS[b]], c == 7)
                X.op("dve", lambda e: e.tensor_tensor(out=xTs[:, dc, :], in0=ps[b][:], in1=xTs[:, dc, :], op=ALU.add), reads=[PS[b], XT], writes=[XT])

        if stop == 7:
            return finish()
        rmsnorm(8)
        ringF = Ring([0, 1, 2, 3, 4, 5])
        for g in range(8):
            si = wload8(w1, g * 512)
            for j in range(4):
                fc = g * 4 + j
                b = ringF.next()
                fm_chunk(si, j, b)
                r = Tg.next()
                X.op("act", lambda e: e.activation(out=r.ap[:], in_=ps[b][:], func=AF.Relu), reads=[PS[b]], writes=[r])
                X.op("dve", lambda e: e.tensor_tensor(out=RBt[:, fc, :], in0=r.ap[:], in1=r.ap[:], op=ALU.mult), reads=[r], writes=[RB[fc]])
        for dc in range(8):
            si = wload([((lambda i, q_=q_: slots[i][:, q_ * 1024:(q_ + 1) * 1024].rearrange("p (f m) -> p f m", f=8)),
                         w2[q_ * 1024:(q_ + 1) * 1024, dc * 128:(dc + 1) * 128].rearrange("(f p) m -> p f m", p=128)) for q_ in range(4)])
            w2v = slots[si][:].rearrange("p (f m) -> p f m", f=32)
            b = ringO.next()
            for fc in range(32):
                mm(ps[b][:], w2v[:, fc, :], RBt[:, fc, :], fc == 0, fc == 31, [SLOT[si], RB[fc]], [PS[b]], fc == 31)
            X.op("dve", lambda e: e.tensor_tensor(out=xTs[:, dc, :], in0=ps[b][:], in1=xTs[:, dc, :], op=ALU.add), reads=[PS[b], XT], writes=[XT])
            X.op("sp", lambda e: e.dma_start(out=outT[dc * 128:(dc + 1) * 128, tsl], in_=xTs[:, dc, :]), reads=[XT], dma=d_o[dc])

    for k in d_o:
        nc.sync.wait_ge(X.sems[k], X.cnt[k])
    return nc


def make_in_maps(inputs, consts):
    x = np.asarray(inputs["x"], dtype=np.float32)
    positions = np.asarray(inputs["positions"]).astype(np.int32)
    B = x.shape[0]
    g = lambda k: np.asarray(inputs[k], dtype=np.float32)[0]
    vecs = np.zeros((128, 28), np.float32)
    vecs[:, 0:8] = g("norm1_w").reshape(8, 128).T
    vecs[:, 8:16] = g("norm2_w").reshape(8, 128).T
    vecs[:, 16] = np.tile(g("q_norm_w"), 2)
    vecs[:, 17] = np.tile(g("k_norm_w"), 2)
    vecs[:, 18] = g("da_subln_w")
    vecs[:, 19] = 1e-6
    vecs[:, 20] = 1e-5
    vecs[:, 21] = np.float32(np.pi / 2)
    vecs[:, 22] = consts["col_da"]
    vecs[:, 23] = consts["col_rt"]
    vecs[:, 24:28] = consts["zeta"]
    lamv = np.concatenate([g("lambda_q1"), g("lambda_k1"), g("lambda_q2"), g("lambda_k2")])[None, :]
    shared = dict(
        w_in=np.ascontiguousarray(g("w_in")), w_a=np.ascontiguousarray(g("w_da_branch")),
        w_r=np.ascontiguousarray(g("w_ret_branch")), w_o=np.ascontiguousarray(g("w_out")),
        w1=np.ascontiguousarray(g("w_mlp_in")), w2=np.ascontiguousarray(g("w_mlp_out")),
        vecs=vecs, gnw=np.ascontiguousarray(g("ret_gn_w")[None, :]), lamv=np.ascontiguousarray(lamv),
        c_mats=consts["c_mats"], c_mask=consts["c_mask"], c_dec=consts["c_dec"],
    )
    in_maps = []
    for b in range(B):
        m = dict(shared)
        m["xT"] = np.ascontiguousarray(x[b].T)
        m["pos"] = np.ascontiguousarray(positions[b][None, :])
        in_maps.append(m)
    return in_maps


def kernel(**inputs):
    x = np.asarray(inputs["x"])
    B, S, _ = x.shape
    consts = host_consts()
    nc = build(S, consts["gamma_c"])
    in_maps = make_in_maps(inputs, consts)
    res = run_bass_kernel_spmd(nc, in_maps, core_ids=list(range(B)))
    out = np.stack([np.ascontiguousarray(res.results[b]["outT"].T) for b in range(B)], axis=0)
    return out.astype(np.float32)
```
